# Optimizing a Trainium2 kernel written in Bass

```python
import math
import jax, jax.numpy as jnp
from jax import lax
import numpy as np

D_MODEL = 1024
BATCH = 16
SEQ = 4096
DEPTH = 1

MEM_LEN = 256
POOL_WIDTH = D_MODEL // 2
POOL_WINDOWS = (2, 4, 8, 16)
POOL_GROUP = POOL_WIDTH // len(POOL_WINDOWS)
ATTN_WIDTH = D_MODEL - POOL_WIDTH
ATTN_HEAD_DIM = 64
ATTN_HEADS = ATTN_WIDTH // ATTN_HEAD_DIM
IN_WIDTH = POOL_WIDTH + 3 * ATTN_WIDTH
MOBA_BLOCK = 256
MOBA_TOPK = 3
Q_CHUNK = 16
XATTN_HEADS = 4
XATTN_HEAD_DIM = D_MODEL // XATTN_HEADS
D_FF = -(-8 * D_MODEL // (3 * 256)) * 256
DEEPNORM_ALPHA = (2.0 * DEPTH) ** 0.25
DEEPNORM_BETA = (8.0 * DEPTH) ** -0.25
LN_EPS = 1e-5

kernel_name = "hymba_pool_moba_deepnorm_layer"


def layer_norm(x, g, b):
    xf = x.astype(jnp.float32)
    mu = jnp.mean(xf, axis=-1, keepdims=True)
    var = jnp.mean(jnp.square(xf - mu), axis=-1, keepdims=True)
    return ((xf - mu) * lax.rsqrt(var + LN_EPS) * g + b).astype(x.dtype)


def multiscale_pool(u, w_pool, pool_scale):
    B_, S_, _ = u.shape
    uf = u.astype(jnp.float32)
    cs = jnp.pad(jnp.cumsum(uf, axis=1), ((0, 0), (1, 0), (0, 0)))
    t = jnp.arange(S_)
    groups = []
    for g, w in enumerate(POOL_WINDOWS):
        sl = slice(g * POOL_GROUP, (g + 1) * POOL_GROUP)
        c_hi = cs[:, 1:, sl]
        c_lo = jnp.pad(cs[:, :S_ + 1 - w, sl], ((0, 0), (w - 1, 0), (0, 0)))
        count = jnp.minimum(t + 1, w).astype(jnp.float32)[None, :, None]
        groups.append((c_hi - c_lo) / count - uf[..., sl])
    pooled = jnp.stack(groups, axis=2)
    mixed = jnp.einsum('bsgc,gcd->bsgd', pooled, w_pool.astype(jnp.float32))
    return (mixed.reshape(B_, S_, POOL_WIDTH) * pool_scale).astype(u.dtype)


def moba_attention(q, k, v):
    B_, S_, H, Dh = q.shape
    nb = -(-S_ // MOBA_BLOCK)
    s_pad = nb * MOBA_BLOCK
    pad = ((0, 0), (0, s_pad - S_), (0, 0), (0, 0))
    q, k, v = [jnp.pad(a, pad).transpose(0, 2, 1, 3) for a in (q, k, v)]
    kb = k.reshape(B_, H, nb, MOBA_BLOCK, Dh)
    vb = v.reshape(B_, H, nb, MOBA_BLOCK, Dh)
    k_mean = jnp.mean(kb.astype(jnp.float32), axis=3)
    topk = min(MOBA_TOPK, nb - 1)
    scale = ATTN_HEAD_DIM ** -0.5
    n_chunks = -(-S_ // Q_CHUNK)
    gather_blocks = jax.vmap(jax.vmap(lambda tab, ix: tab[ix]))

    def chunk_fn(ci):
        q0 = ci * Q_CHUNK
        blk = q0 // MOBA_BLOCK
        qc = lax.dynamic_slice_in_dim(q, q0, Q_CHUNK, axis=2)
        k_own = lax.dynamic_slice_in_dim(k, blk * MOBA_BLOCK, MOBA_BLOCK, axis=2)
        v_own = lax.dynamic_slice_in_dim(v, blk * MOBA_BLOCK, MOBA_BLOCK, axis=2)
        qpos = q0 + jnp.arange(Q_CHUNK)
        kpos = blk * MOBA_BLOCK + jnp.arange(MOBA_BLOCK)
        s_own = jnp.einsum('bhqd,bhkd->bhqk', qc, k_own,
                           preferred_element_type=jnp.float32) * scale
        s_own = jnp.where(kpos[None, :] <= qpos[:, None], s_own, -jnp.inf)
        if topk > 0:
            gate = jnp.einsum('bhqd,bhnd->bhqn', qc.astype(jnp.float32), k_mean)
            gate = jnp.where(jnp.arange(nb) < blk, gate, -jnp.inf)
            _, idx = lax.top_k(gate, topk)
            valid = idx < blk
            k_sel = gather_blocks(kb, idx)
            v_sel = gather_blocks(vb, idx)
            s_sel = jnp.einsum('bhqd,bhqnkd->bhqnk', qc, k_sel,
                               preferred_element_type=jnp.float32) * scale
            s_sel = jnp.where(valid[..., None], s_sel, -jnp.inf)
            s_sel = s_sel.reshape(B_, H, Q_CHUNK, topk * MOBA_BLOCK)
            p = jax.nn.softmax(jnp.concatenate([s_sel, s_own], axis=-1), axis=-1)
            p_sel = p[..., :topk * MOBA_BLOCK].reshape(B_, H, Q_CHUNK, topk, MOBA_BLOCK)
            p_own = p[..., topk * MOBA_BLOCK:]
            o = (jnp.einsum('bhqnk,bhqnkd->bhqd', p_sel.astype(v.dtype), v_sel,
                            preferred_element_type=jnp.float32)
                 + jnp.einsum('bhqk,bhkd->bhqd', p_own.astype(v.dtype), v_own,
                              preferred_element_type=jnp.float32))
        else:
            p_own = jax.nn.softmax(s_own, axis=-1)
            o = jnp.einsum('bhqk,bhkd->bhqd', p_own.astype(v.dtype), v_own,
                           preferred_element_type=jnp.float32)
        return o.astype(q.dtype)

    out = lax.map(chunk_fn, jnp.arange(n_chunks))
    out = out.transpose(1, 0, 3, 2, 4).reshape(B_, n_chunks * Q_CHUNK, H, Dh)
    return out[:, :S_]


def memory_cross_attention(h, mem, w_xq, w_xkv, w_xo):
    B_, S_, D = h.shape
    M = mem.shape[1]
    q = (h @ w_xq).reshape(B_, S_, XATTN_HEADS, XATTN_HEAD_DIM)
    kv = mem @ w_xkv
    k = kv[..., :D].reshape(B_, M, XATTN_HEADS, XATTN_HEAD_DIM)
    v = kv[..., D:].reshape(B_, M, XATTN_HEADS, XATTN_HEAD_DIM)
    s = jnp.einsum('bshd,bmhd->bhsm', q, k,
                   preferred_element_type=jnp.float32) * (XATTN_HEAD_DIM ** -0.5)
    p = jax.nn.softmax(s, axis=-1)
    o = jnp.einsum('bhsm,bmhd->bshd', p.astype(v.dtype), v).reshape(B_, S_, D)
    return o @ w_xo


def swiglu(h, w_gate, w_up, w_down):
    return (jax.nn.silu(h @ w_gate) * (h @ w_up)) @ w_down


def setup_inputs(seed: int = 0) -> dict:
    key = jax.random.key(seed)
    ks = jax.random.split(key, 20)
    f32 = jnp.float32
    nrm = lambda k, shape, s: jax.random.normal(k, shape, f32) * s
    L, D = DEPTH, D_MODEL
    return {
        "x": nrm(ks[0], (BATCH, SEQ, D), 1.0),
        "mem": nrm(ks[1], (BATCH, MEM_LEN, D), 1.0),
        "w_in": nrm(ks[2], (L, D, IN_WIDTH), D ** -0.5),
        "w_pool": nrm(ks[3], (L, len(POOL_WINDOWS), POOL_GROUP, POOL_GROUP), POOL_GROUP ** -0.5),
        "pool_scale": 1.0 + nrm(ks[4], (L, POOL_WIDTH), 0.02),
        "w_out": nrm(ks[5], (L, D, D), D ** -0.5 * DEEPNORM_BETA),
        "ln1_g": 1.0 + nrm(ks[6], (L, D), 0.02),
        "ln1_b": nrm(ks[7], (L, D), 0.02),
        "w_xq": nrm(ks[8], (L, D, D), D ** -0.5),
        "w_xkv": nrm(ks[9], (L, D, 2 * D), D ** -0.5),
        "w_xo": nrm(ks[10], (L, D, D), D ** -0.5 * DEEPNORM_BETA),
        "ln2_g": 1.0 + nrm(ks[11], (L, D), 0.02),
        "ln2_b": nrm(ks[12], (L, D), 0.02),
        "w_gate": nrm(ks[13], (L, D, D_FF), D ** -0.5),
        "w_up": nrm(ks[14], (L, D, D_FF), D ** -0.5),
        "w_down": nrm(ks[15], (L, D_FF, D), D_FF ** -0.5 * DEEPNORM_BETA),
        "ln3_g": 1.0 + nrm(ks[16], (L, D), 0.02),
        "ln3_b": nrm(ks[17], (L, D), 0.02),
    }


def reference(x, mem, w_in, w_pool, pool_scale, w_out, ln1_g, ln1_b, w_xq, w_xkv, w_xo,
              ln2_g, ln2_b, w_gate, w_up, w_down, ln3_g, ln3_b):
    B_, S_, _ = x.shape
    h = x
    for l in range(DEPTH):
        z = h @ w_in[l]
        u = z[..., :POOL_WIDTH]
        q = z[..., POOL_WIDTH:POOL_WIDTH + ATTN_WIDTH].reshape(B_, S_, ATTN_HEADS, ATTN_HEAD_DIM)
        k = z[..., POOL_WIDTH + ATTN_WIDTH:POOL_WIDTH + 2 * ATTN_WIDTH].reshape(B_, S_, ATTN_HEADS, ATTN_HEAD_DIM)
        v = z[..., POOL_WIDTH + 2 * ATTN_WIDTH:].reshape(B_, S_, ATTN_HEADS, ATTN_HEAD_DIM)
        pool_out = multiscale_pool(u, w_pool[l], pool_scale[l])
        attn_out = moba_attention(q, k, v).reshape(B_, S_, ATTN_WIDTH)
        mix = jnp.concatenate([pool_out, attn_out], axis=-1) @ w_out[l]
        h = layer_norm(DEEPNORM_ALPHA * h + mix, ln1_g[l], ln1_b[l])
        h = layer_norm(DEEPNORM_ALPHA * h + memory_cross_attention(h, mem, w_xq[l], w_xkv[l], w_xo[l]),
                       ln2_g[l], ln2_b[l])
        h = layer_norm(DEEPNORM_ALPHA * h + swiglu(h, w_gate[l], w_up[l], w_down[l]),
                       ln3_g[l], ln3_b[l])
    return h
```

```python
import numpy as np
import ml_dtypes
from contextlib import ExitStack
import concourse.bass as bass
import concourse.mybir as mybir
from concourse.bass_utils import run_bass_kernel_spmd

F32 = mybir.dt.float32
BF16 = mybir.dt.bfloat16
AF = mybir.ActivationFunctionType
ALU = mybir.AluOpType
AX = mybir.AxisListType

D = 1024
DFF = 2816
NF = DFF // 128
NFP = NF // 2
MEM = 256
T = 256
ALPHA = 2.0 ** 0.25
EPS = 1e-5
NEG = -30000.0
NSLOT = 10
EPOCH_TILES = 4
import os
STOP = int(os.environ.get('KSTOP', '99'))


class Op:
    __slots__ = ("eng", "fn", "deps", "key", "kidx", "epoch", "signal", "cnt", "idx", "tag")


class Sched:
    ENGS = ("pe", "act", "dve", "pool", "sp")

    def __init__(self):
        self.ops = []
        self.last_w = {}
        self.readers = {}
        self.epoch = 0
        self.key_count = {}

    def op(self, eng, fn, reads=(), writes=(), key=None):
        o = Op()
        o.eng, o.fn, o.key, o.epoch = eng, fn, key, self.epoch
        o.tag = getattr(self, "tag", "")
        o.idx = len(self.ops)
        o.signal = False
        o.cnt = 0
        deps = set()
        psr = [r for r in reads if isinstance(r, tuple) and r[0] == "ps"]
        if psr:
            reads = [r for r in reads if r not in psr]
            writes = list(writes) + psr
        for r in reads:
            w = self.last_w.get(r)
            if w is not None:
                deps.add(w)
        for r in writes:
            w = self.last_w.get(r)
            if w is not None:
                deps.add(w)
            for rd in self.readers.get(r, ()):
                deps.add(rd)
        deps.discard(o.idx)
        o.deps = deps
        for r in reads:
            self.readers.setdefault(r, []).append(o.idx)
        for r in writes:
            self.last_w[r] = o.idx
            self.readers[r] = []
        if key is not None:
            self.key_count[key] = self.key_count.get(key, 0) + 1
            o.kidx = self.key_count[key]
        else:
            o.kidx = 0
        self.ops.append(o)
        return o

    def finalize(self):
        ops = self.ops
        for o in ops:
            for d in o.deps:
                a = ops[d]
                if a.key is None and not (a.eng == "pe" and o.eng == "pe"):
                    a.signal = True
        cnt = {}
        for o in ops:
            if o.key is None and o.signal:
                k = (o.eng, o.epoch)
                cnt[k] = cnt.get(k, 0) + 1
                o.cnt = cnt[k]
        self.nepoch = self.epoch + 1

    def emit(self, nc, es, final_waits):
        ops = self.ops
        engsem = {}
        for e in self.ENGS:
            for ep in range(self.nepoch):
                engsem[(e, ep)] = es.enter_context(nc.semaphore(f"s_{e}_{ep}"))
        keysem = {}
        for k in self.key_count:
            keysem[k] = es.enter_context(nc.semaphore(f"k_{k}"))
        per_eng = {e: [o for o in ops if o.eng == e] for e in self.ENGS}
        block = es.enter_context(nc.Block())

        def run_engine(ename, eng, extra_final=None):
            waited = {}
            maxep = {}
            for o in per_eng[ename]:
                need = {}
                for d in o.deps:
                    a = ops[d]
                    if a.key is not None:
                        sk = ("k", a.key)
                        need[sk] = max(need.get(sk, 0), 16 * a.kidx)
                    else:
                        if a.eng == "pe" and ename == "pe":
                            continue
                        if maxep.get(a.eng, -1) > a.epoch:
                            continue
                        sk = ("e", a.eng, a.epoch)
                        need[sk] = max(need.get(sk, 0), a.cnt)
                for sk, v in need.items():
                    if waited.get(sk, 0) >= v:
                        continue
                    waited[sk] = v
                    if sk[0] == "k":
                        eng.wait_ge(keysem[sk[1]], v)
                    else:
                        eng.wait_ge(engsem[(sk[1], sk[2])], v)
                        maxep[sk[1]] = max(maxep.get(sk[1], -1), sk[2])
                ins = o.fn(eng)
                if o.key is not None:
                    ins.then_inc(keysem[o.key], 16)
                elif o.signal:
                    ins.then_inc(engsem[(ename, o.epoch)], 1)
            if extra_final:
                for a in extra_final:
                    eng.wait_ge(keysem[a.key], 16 * a.kidx)

        @block.tensor
        def _(e):
            run_engine("pe", e)

        @block.scalar
        def _(e):
            run_engine("act", e)

        @block.vector
        def _(e):
            run_engine("dve", e)

        @block.gpsimd
        def _(e):
            run_engine("pool", e, extra_final=final_waits)

        @block.sync
        def _(e):
            run_engine("sp", e)


def build(NB, S):
    NBLK = S // T
    nc = bass.Bass("TRN2", target_bir_lowering=False)
    es = ExitStack()
    sc = Sched()

    def dram_in(name, shape, dt=F32):
        return nc.dram_tensor(name, list(shape), dt, kind="ExternalInput").ap()

    x_d = dram_in("x", [NB, S, D])
    mem_d = dram_in("mem", [NB, MEM, D])
    w_in_d = dram_in("w_in", [D, 2048])
    w_pool_d = dram_in("w_pool", [4, 128, 128])
    w_out_d = dram_in("w_out", [D, D])
    w_xq_d = dram_in("w_xq", [D, D])
    w_xkv_d = dram_in("w_xkv", [D, 2 * D])
    w_xo_d = dram_in("w_xo", [D, D])
    w_gate_d = dram_in("w_gate", [D, DFF])
    w_up_d = dram_in("w_up", [D, DFF])
    w_down_d = dram_in("w_down", [DFF, D])
    lnp_d = dram_in("lnp", [6, 128, D])
    pscale_d = dram_in("pscale", [128, 4])
    cb_d = dram_in("cbf", [128, 128 * 3 + 2048], BF16)
    cf_d = dram_in("cf32", [128, 128 + 64 + 1])
    out_d = nc.dram_tensor("out", [NB, S, D], F32, kind="ExternalOutput").ap()

    def dram_scr(name, shape):
        return nc.dram_tensor(name, list(shape), BF16, kind="Internal").ap()

    wb = {
        "in": dram_scr("wb_in", [D, 2048]),
        "out": dram_scr("wb_out", [D, D]),
        "xq": dram_scr("wb_xq", [D, D]),
        "xkv": dram_scr("wb_xkv", [D, 2 * D]),
        "xo": dram_scr("wb_xo", [D, D]),
        "gate": dram_scr("wb_gate", [D, DFF]),
        "up": dram_scr("wb_up", [D, DFF]),
        "down": dram_scr("wb_down", [DFF, D]),
    }
    wsrc = {"in": w_in_d, "out": w_out_d, "xq": w_xq_d, "xkv": w_xkv_d, "xo": w_xo_d,
            "gate": w_gate_d, "up": w_up_d, "down": w_down_d}

    def sb(name, shape, dt):
        return es.enter_context(nc.sbuf_tensor(name, list(shape), dt))

    ring = sb("ring", [128, NSLOT, 1024], BF16)
    Kc = sb("Kc", [128, 4, S], BF16)
    Vc = sb("Vc", [128, 2 * NBLK, 768], BF16)
    hres = sb("hres", [128, 2, 2, D], F32)
    hb = sb("hb", [128, 2, D], BF16)
    hT = sb("hT", [128, 8, T], BF16)
    xT = sb("xT", [128, 8, T], BF16)
    Qz = sb("Qz", [128, 8, T], BF16)
    catT = sb("catT", [128, 8, T], BF16)
    U = sb("U", [128, 4, 16 + T], F32)
    pa = sb("pa", [128, 16 + T], F32)
    pb = sb("pb", [128, 16 + T], F32)
    ptmp = sb("ptmp", [128, 16], F32)
    pooledT = sb("pooledT", [128, 4, T], BF16)
    wpool = sb("wpool", [128, 4, 128], BF16)
    PT = sb("PT", [128, 3, 512], BF16)
    km = sb("km", [128, 4, 16], BF16)
    kmf = sb("kmf", [128, 4], F32)
    gsc = sb("gsc", [128, 3, 8, 16], F32)
    gm = sb("gm", [128, 3, 8], F32)
    mb = sb("mb", [128, 8, 128], BF16)
    mbT = sb("mbT", [128, 8, T], BF16)
    XQ = mbT
    accsb = sb("accsb", [128, 2, T], F32)
    rec = sb("rec", [128, 2, T], F32)
    KmT = sb("KmT", [128, 8, MEM], BF16)
    Vm = sb("Vm", [128, 2, D], BF16)
    PxT = sb("PxT", [128, 2, 512], BF16)
    sg = sb("sg", [128, 2, T], F32)
    actT = sb("actT", [128, 3, T], BF16)
    lnp = sb("lnp_sb", [128, 6, D], F32)
    pscale = sb("pscale_sb", [128, 4], F32)
    cb = sb("cb", [128, 128 * 3 + 2048], BF16)
    cf = sb("cf", [128, 128 + 64 + 1], F32)
    stats = sb("stats", [128, 2, 2, 6], F32)
    mv = sb("mv", [128, 2, 2], F32)
    rstd = sb("rstd", [128, 2], F32)
    stage = sb("stage", [128, 2, 128], F32)

    ident = cb[:, 0:128]
    tri = cb[:, 128:256]
    ones = cb[:, 256:384]
    Wsel = cb[:, 384:384 + 2048]
    perm = cf[:, 0:128]
    invcnt = cf[:, 128:192]
    neghalf = cf[:, 192:193]

    ps = [es.enter_context(nc.psum_tensor(f"ps{b}", [128, 512], F32)) for b in range(8)]

    def psb(b):
        return ("ps", b)

    def ps_bf(b):
        return ps[b][:, :].bitcast(BF16)

    def dma(eng, out, in_, reads, writes, key):
        return sc.op(eng, lambda e: e.dma_start(out=out, in_=in_), reads, writes, key)

    dma("sp", cb[:, :], cb_d[:, :], [], ["cb"], "c0")
    dma("sp", cf[:, :], cf_d[:, :], [], ["cf"], "c1")
    dma("sp", lnp[:, :, :], lnp_d.rearrange("k p d -> p k d"), [], ["lnp"], "c2")
    dma("sp", pscale[:, :], pscale_d[:, :], [], ["pscale"], "c3")
    for nm in ("in", "out", "xq", "xkv", "xo", "gate", "up", "down"):
        src = wsrc[nm].rearrange("(p a) n -> p a n", p=128)
        dst = wb[nm].rearrange("(p a) n -> p a n", p=128)
        dma("pool", dst, src, [], [("wb", nm)], "wc_" + nm)
    wbU = {}
    wbU_done = set()
    for nm, ncol in (("in", 1536), ("xq", D), ("gate", DFF), ("up", DFF)):
        wbU[nm] = dram_scr("wbU_" + nm, [ncol // 128, 128, 1024])

    def ensure_relayout(nm):
        if nm in wbU_done:
            return
        wbU_done.add(nm)
        for m in range(wbU[nm].shape[0]):
            src = wb[nm][:, m * 128:(m + 1) * 128].rearrange("(c p) n -> p c n", p=128)
            dst = wbU[nm][m].rearrange("p (c n) -> p c n", n=128)
            dma("pool", dst, src, [("wb", nm)], [("wbU", nm, m), ("rukey", m % 4)], f"ru{m % 4}")

    ensure_relayout("in")
    for g in range(4):
        dma("sp", stage[:, g % 2, :], w_pool_d[g, :, :], [], [("stage", g % 2)], f"st{g % 2}")
        sc.op("act", lambda e, g=g: e.activation(out=wpool[:, g, :], in_=stage[:, g % 2, :], func=AF.Copy),
              [("stage", g % 2)], [("wpool", g)])
    sc.op("dve", lambda e: e.memset(Qz[:, :, :], 0.0), [], [("Qz", h) for h in range(8)])
    sc.op("dve", lambda e: e.memset(mb[:, :, :], 0.0), [], ["mb"])
    sc.op("dve", lambda e: e.memset(km[:, :, :], 0.0), [], ["km"])
    sc.op("pool", lambda e: e.memset(Vc[:, :, :], 1.0), [], [("Vc", j) for j in range(NBLK)])

    ring_ctr = [0]

    def stream(src_ap, shape_view, wname, res=None):
        u = ring_ctr[0]
        ring_ctr[0] += 1
        slot = u % NSLOT
        dst = shape_view(ring[:, slot, :])
        dma("sp", dst, src_ap, [res if res is not None else ("wb", wname)], [("ring", slot)], f"r{slot}")
        return slot

    def lhs_unit(wname, m):
        if wname in wbU:
            ensure_relayout(wname)
            slot = stream(wbU[wname][m], lambda r: r[:, 0:1024], wname, res=("wbU", wname, m))
        else:
            src = wb[wname][:, m * 128:(m + 1) * 128].rearrange("(c p) n -> p c n", p=128)
            slot = stream(src, lambda r: r[:, 0:1024].rearrange("p (c n) -> p c n", n=128), wname)
        view = ring[:, slot, 0:1024].rearrange("p (c n) -> p c n", n=128)
        return slot, view

    def rhs_unit(wname, row0, nrows_chunks, col0, ncols):
        assert nrows_chunks * ncols <= 1024
        src = wb[wname][row0:row0 + nrows_chunks * 128, col0:col0 + ncols].rearrange("(k p) n -> p k n", p=128)
        slot = stream(src, lambda r: r[:, 0:nrows_chunks * ncols].rearrange("p (k n) -> p k n", n=ncols), wname)
        view = ring[:, slot, 0:nrows_chunks * ncols].rearrange("p (k n) -> p k n", n=ncols)
        return slot, view

    def mm(out, lhsT, rhs, start, stop, reads, writes):
        return sc.op("pe", lambda e: e.matmul(out, lhsT, rhs, start=start, stop=stop), reads, writes)

    def transpose_to_hT(src_bf, src_res, dst, dst_res_fn, bank, evac="dve"):
        for s in range(2):
            pv = ps_bf(bank[s])
            for c in range(8):
                sc.op("pe", lambda e, s=s, c=c, pv=pv: e.transpose(pv[:, c * 128:(c + 1) * 128],
                                                                    src_bf[:, s, c * 128:(c + 1) * 128], ident),
                      [src_res(s), "cb"], [psb(bank[s])])
            if evac == "dve":
                sc.op("dve", lambda e, s=s, pv=pv: e.tensor_copy(
                    out=dst[:, :, s * 128:(s + 1) * 128],
                    in_=pv[:, :].rearrange("p (c t) -> p c t", t=128)),
                    [psb(bank[s])], [dst_res_fn(s)])
            else:
                sc.op("act", lambda e, s=s, pv=pv: e.activation(
                    out=dst[:, :, s * 128:(s + 1) * 128],
                    in_=pv[:, :].rearrange("p (c t) -> p c t", t=128), func=AF.Copy),
                    [psb(bank[s])], [dst_res_fn(s)])

    def layer_norm(hbuf, ybanks, k, out_store=None):
        for s in range(2):
            hr = hres[:, hbuf, s, :]
            for half in range(2):
                b = ybanks[s][half]
                sc.op("dve", lambda e, hr=hr, half=half, b=b: e.scalar_tensor_tensor(
                    out=hr[:, half * 512:(half + 1) * 512], in0=hr[:, half * 512:(half + 1) * 512],
                    scalar=ALPHA, in1=ps[b][:, :], op0=ALU.mult, op1=ALU.add),
                    [("hres", hbuf, s), psb(b)], [("hres", hbuf, s)])
            for half in range(2):
                sc.op("dve", lambda e, hr=hr, half=half, s=s: e.bn_stats(
                    out=stats[:, s, half, :], in_=hr[:, half * 512:(half + 1) * 512]),
                    [("hres", hbuf, s)], [("stats", s, half)])
            sc.op("dve", lambda e, s=s: e.bn_aggr(out=mv[:, s, :], in_=stats[:, s, :, :].rearrange("p a b -> p (a b)")),
                  [("stats", s, 0), ("stats", s, 1)], [("mv", s)])
            sc.op("dve", lambda e, s=s: e.tensor_scalar(out=rstd[:, s:s + 1], in0=mv[:, s, 1:2], scalar1=EPS,
                                                        scalar2=None, op0=ALU.add),
                  [("mv", s)], [("rstd", s)])
            sc.op("pool", lambda e, s=s: e.tensor_tensor(out=rstd[:, s:s + 1], in0=rstd[:, s:s + 1], in1=neghalf,
                                                         op=ALU.pow),
                  [("rstd", s), "cf"], [("rstd", s)])
            sc.op("dve", lambda e, hr=hr, s=s: e.tensor_scalar(out=hr, in0=hr, scalar1=mv[:, s, 0:1],
                                                               scalar2=rstd[:, s:s + 1], op0=ALU.subtract,
                                                               op1=ALU.mult),
                  [("hres", hbuf, s), ("mv", s), ("rstd", s)], [("hres", hbuf, s)])
            sc.op("pool", lambda e, hr=hr: e.tensor_tensor(out=hr, in0=hr, in1=lnp[:, 2 * k, :], op=ALU.mult),
                  [("hres", hbuf, s), "lnp"], [("hres", hbuf, s)])
            sc.op("dve", lambda e, hr=hr: e.tensor_tensor(out=hr, in0=hr, in1=lnp[:, 2 * k + 1, :], op=ALU.add),
                  [("hres", hbuf, s), "lnp"], [("hres", hbuf, s)])
            if out_store is None:
                sc.op("pool", lambda e, hr=hr, s=s: e.tensor_copy(out=hb[:, s, :], in_=hr),
                      [("hres", hbuf, s)], [("hb", s)])

    def token_major_proj(lhs_buf, lhs_res, wname, ybanks):
        for k in range(8):
            slot, view = rhs_unit(wname, k * 128, 1, 0, D)
            for s in range(2):
                for half in range(2):
                    mm(ps[ybanks[s][half]][:, :], lhs_buf[:, k, s * 128:(s + 1) * 128],
                       view[:, 0, half * 512:(half + 1) * 512], k == 0, k == 7,
                       [lhs_res(k), ("ring", slot)], [psb(ybanks[s][half])])

    YB = [[2, 3], [5, 6]]

    VB = [4, 7]

    def phase1(b, i, hbuf):
        sc.tag = 'p1'
        for s in range(2):
            sc.op("act", lambda e, s=s, hbuf=hbuf: e.activation(out=hb[:, s, :], in_=hres[:, hbuf, s, :], func=AF.Copy),
                  [("hres", hbuf, s)], [("hb", s)])
        transpose_to_hT(hb, lambda s: ("hb", s), xT, lambda s: ("xT", s), [0, 1], evac="act")

    def phase2(b, i, hbuf, part):
        sc.tag = 'p2'
        for m in (range(0, 8) if part == 0 else range(8, 12)):
            slot, view = lhs_unit("in", m)
            if True:
                bank = m % 2
                for c in range(8):
                    mm(ps[bank][:, 0:T], view[:, c, :], xT[:, c, :], c == 0, c == 7,
                       [("xT", 0), ("xT", 1), ("ring", slot)], [psb(bank)])
                if m < 4:
                    sc.op("act", lambda e, m=m, bank=bank: e.activation(out=U[:, m, 16:16 + T], in_=ps[bank][:, 0:T], func=AF.Copy),
                          [psb(bank)], [("U", m)])
                elif m < 8:
                    p = m - 4
                    sc.op("act", lambda e, p=p, bank=bank: e.activation(out=Qz[0:64, 2 * p, :], in_=ps[bank][0:64, 0:T], func=AF.Copy, scale=0.125),
                          [psb(bank)], [("Qz", 2 * p)])
                    sc.op("act", lambda e, p=p, bank=bank: e.activation(out=Qz[64:128, 2 * p + 1, :], in_=ps[bank][64:128, 0:T], func=AF.Copy, scale=0.125),
                          [psb(bank)], [("Qz", 2 * p + 1)])
                else:
                    p = m - 8
                    sc.op("act", lambda e, p=p, bank=bank, i=i: e.activation(
                        out=Kc[:, p, i * T:(i + 1) * T], in_=ps[bank][:, 0:T], func=AF.Copy, accum_out=kmf[:, p:p + 1]),
                        [psb(bank)], [("Kc", p, i), ("kmf", p)])
                    sc.op("pool", lambda e, p=p, i=i: e.tensor_scalar(out=km[:, p, i:i + 1], in0=kmf[:, p:p + 1], scalar1=1.0 / T,
                                                                      scalar2=None, op0=ALU.mult),
                          [("kmf", p)], ["km"])
        if part == 0:
            return
        for u2 in range(4):
            slot, view = rhs_unit("in", u2 * 256, 2, 1536, 512)
            for kk in range(2):
                k = u2 * 2 + kk
                for s in range(2):
                    mm(ps[VB[s]][:, :], xT[:, k, s * 128:(s + 1) * 128], view[:, kk, :], k == 0, k == 7,
                       [("xT", s), ("ring", slot)], [psb(VB[s])])
        for s in range(2):
            kt = 2 * i + s
            vdst = Vc[:, kt, :].rearrange("p (pr x) -> p pr x", x=192)
            vsrc = ps[VB[s]][:, :].rearrange("p (pr two d) -> p pr two d", two=2, d=64)
            sc.op("act", lambda e, vdst=vdst, vsrc=vsrc: e.activation(out=vdst[:, :, 0:64], in_=vsrc[:, :, 0, :], func=AF.Copy),
                  [psb(VB[s])], [("Vc", i)])
            sc.op("act", lambda e, vdst=vdst, vsrc=vsrc: e.activation(out=vdst[:, :, 128:192], in_=vsrc[:, :, 1, :], func=AF.Copy),
                  [psb(VB[s])], [("Vc", i)])

    def phase34(i):
        if i == 0:
            sc.op("pool", lambda e: e.memset(U[:, :, 0:16], 0.0), [], [("U", g) for g in range(4)])
        sc.tag = 'p3'
        for g in range(4):
            src = U[:, g, :]
            bufs = [pa, pb]
            cur = src
            sh = 1
            lo = 0
            for lvl in range(g + 1):
                dst = bufs[lvl % 2]
                lo2 = lo + sh
                sc.op("pool", lambda e, dst=dst, cur=cur, sh=sh, lo2=lo2: e.tensor_tensor(
                    out=dst[:, lo2:16 + T], in0=cur[:, lo2:16 + T], in1=cur[:, lo2 - sh:16 + T - sh], op=ALU.add),
                    [("U", g), "pa", "pb"], ["pa" if lvl % 2 == 0 else "pb"])
                cur = dst
                lo = lo2
                sh *= 2
            w = 2 ** (g + 1)
            curname = "pa" if g % 2 == 0 else "pb"
            sc.op("dve", lambda e, cur=cur, g=g, w=w: e.scalar_tensor_tensor(
                out=pooledT[:, g, :], in0=cur[:, 16:16 + T], scalar=1.0 / w, in1=U[:, g, 16:16 + T],
                op0=ALU.mult, op1=ALU.subtract),
                [curname, ("U", g)], [("pooledT", g)])
            if i == 0:
                sc.op("pool", lambda e, cur=cur, g=g: e.tensor_tensor(
                    out=ptmp[:, :], in0=cur[:, 16:32], in1=invcnt[:, g * 16:(g + 1) * 16], op=ALU.mult),
                    [curname, "cf"], ["ptmp"])
                sc.op("pool", lambda e, g=g: e.tensor_tensor(
                    out=pooledT[:, g, 0:16], in0=ptmp[:, :], in1=U[:, g, 16:32], op=ALU.subtract),
                    ["ptmp", ("U", g)], [("pooledT", g)])
            sc.op("pool", lambda e, g=g: e.tensor_copy(out=U[:, g, 0:16], in_=U[:, g, T:T + 16]),
                  [("U", g), curname], [("U", g)])
            bank = g % 2
            mm(ps[bank][:, 0:T], wpool[:, g, :], pooledT[:, g, :], True, True,
               [("wpool", g), ("pooledT", g)], [psb(bank)])
            sc.op("dve", lambda e, g=g, bank=bank: e.tensor_scalar(out=catT[:, g, :], in0=ps[bank][:, 0:T],
                                                                  scalar1=pscale[:, g:g + 1], scalar2=None, op0=ALU.mult),
                  [psb(bank), "pscale"], [("catT", g)])

        sc.tag = 'p4'
        use_mask = i > 3
        if use_mask:
            for s in range(2):
                for h in range(8):
                    mm(ps[4][:, s * 128 + h * 16: s * 128 + h * 16 + 16], Qz[:, h, s * 128:(s + 1) * 128],
                       km[:, h // 2, :], True, True, [("Qz", h), "km"], [psb(4)])
                G = ps[4][:, s * 128:(s + 1) * 128].rearrange("p (h n) -> p h n", n=16)[:, :, 0:i]
                A = gsc[:, 0, :, 0:i]
                B = gsc[:, 1, :, 0:i]
                C = gsc[:, 2, :, 0:i]

                def bc(k_, i=i):
                    return gm[:, k_, :].unsqueeze(2).broadcast_to([128, 8, i])

                gr = ["gsc", "gm", psb(4)]
                sc.op("dve", lambda e, G=G: e.tensor_reduce(out=gm[:, 0, :], in_=G, axis=AX.X, op=ALU.max), gr, ["gm"])
                sc.op("dve", lambda e, G=G, A=A, bc=bc: e.tensor_tensor(out=A, in0=G, in1=bc(0), op=ALU.is_ge), gr, ["gsc"])
                sc.op("dve", lambda e, G=G, A=A, B=B: e.scalar_tensor_tensor(out=B, in0=A, scalar=-1e30, in1=G, op0=ALU.mult, op1=ALU.add), gr, ["gsc"])
                sc.op("dve", lambda e, B=B: e.tensor_reduce(out=gm[:, 1, :], in_=B, axis=AX.X, op=ALU.max), gr, ["gm"])
                sc.op("dve", lambda e, A=A, B=B, bc=bc: e.tensor_tensor(out=A, in0=B, in1=bc(1), op=ALU.is_ge), gr, ["gsc"])
                sc.op("dve", lambda e, A=A, B=B, C=C: e.scalar_tensor_tensor(out=C, in0=A, scalar=-1e30, in1=B, op0=ALU.mult, op1=ALU.add), gr, ["gsc"])
                sc.op("dve", lambda e, C=C: e.tensor_reduce(out=gm[:, 2, :], in_=C, axis=AX.X, op=ALU.max), gr, ["gm"])
                sc.op("dve", lambda e, G=G, A=A, bc=bc: e.tensor_tensor(out=A, in0=G, in1=bc(2), op=ALU.is_ge), gr, ["gsc"])
                sc.op("dve", lambda e, A=A, i=i: e.tensor_scalar(out=mb[:, :, 0:i], in0=A, scalar1=-NEG, scalar2=NEG,
                                                               op0=ALU.mult, op1=ALU.add), gr, ["mb"])
                pv = ps_bf(7)
                for h in range(8):
                    sc.op("pe", lambda e, h=h, pv=pv: e.transpose(pv[:, h * 128:(h + 1) * 128], mb[:, h, :], ident),
                          ["mb", "cb"], [psb(7)])
                sc.op("act", lambda e, s=s, pv=pv: e.activation(
                    out=mbT[:, :, s * 128:(s + 1) * 128], in_=pv[:, :].rearrange("p (h t) -> p h t", t=128), func=AF.Copy),
                    [psb(7)], [("mbT", s)])

    out_ops = []
    g_tile = 0

    def load_x(b, i, hbuf):
        src = x_d[b, i * T:(i + 1) * T, :].rearrange("(s p) d -> p s d", p=128)
        dma("pool", hres[:, hbuf, :, :], src, [], [("hres", hbuf, 0), ("hres", hbuf, 1)], f"x{hbuf}")

    load_x(0, 0, 0)
    sc.tag = 'p1'
    P12_EARLY = True
    for b in range(NB):
        sc.epoch = (g_tile // EPOCH_TILES) + 1 if g_tile > 0 else 0
        sc.tag = 'mem'
        mbuf = (g_tile + 1) % 2
        if STOP >= 1:
            dma("pool", hres[:, mbuf, :, :], mem_d[b, :, :].rearrange("(s p) d -> p s d", p=128),
                [], [("hres", mbuf, 0), ("hres", mbuf, 1)], f"x{mbuf}")
            for s in range(2):
                sc.op("act", lambda e, s=s, mbuf=mbuf: e.activation(out=hb[:, s, :], in_=hres[:, mbuf, s, :], func=AF.Copy),
                      [("hres", mbuf, s)], [("hb", s)])
            transpose_to_hT(hb, lambda s: ("hb", s), hT, lambda s: ("hT", s), [0, 1])
            hT_all = [("hT", 0), ("hT", 1)]
            for m in range(8):
                slot, view = lhs_unit("xkv", m)
                if True:
                    bank = m % 2
                    for c in range(8):
                        mm(ps[bank][:, 0:MEM], view[:, c, :], hT[:, c, :], c == 0, c == 7,
                           hT_all + [("ring", slot)], [psb(bank)])
                    sc.op("act", lambda e, m=m, bank=bank: e.activation(out=KmT[:, m, :], in_=ps[bank][:, 0:MEM], func=AF.Copy),
                          [psb(bank)], [("KmT", m)])
            for k in range(8):
                slot, view = rhs_unit("xkv", k * 128, 1, D, D)
                if True:
                    for s in range(2):
                        for half in range(2):
                            mm(ps[YB[s][half]][:, :], hT[:, k, s * 128:(s + 1) * 128],
                               view[:, 0, half * 512:(half + 1) * 512], k == 0, k == 7,
                               [("hT", s), ("ring", slot)], [psb(YB[s][half])])
            for s in range(2):
                for half in range(2):
                    sc.op("act", lambda e, s=s, half=half: e.activation(
                        out=Vm[:, s, half * 512:(half + 1) * 512], in_=ps[YB[s][half]][:, :], func=AF.Copy),
                        [psb(YB[s][half])], [("Vm", s)])

        for i in range(NBLK):
            sc.epoch = g_tile // EPOCH_TILES + 1
            hbuf = g_tile % 2
            nxt = (b, i + 1) if i + 1 < NBLK else ((b + 1, 0) if b + 1 < NB else None)

            if nxt is not None:
                load_x(nxt[0], nxt[1], (g_tile + 1) % 2)

            if g_tile == 0:
                sc.tag = 'p2'
                phase1(b, i, hbuf)
                phase2(b, i, hbuf, 0)
                phase2(b, i, hbuf, 1)
                phase34(i)

            def body():
                if STOP < 5:
                    return
                sc.tag = 'p5'
                use_mask = i > 3
                items = []
                for h in range(8):
                    blocks = [(i, True)] + [(j, False) for j in range(i)]
                    for bi, (j, own) in enumerate(blocks):
                        items.append((h, j, own, bi == 0, bi == len(blocks) - 1))
                nit = len(items)

                def emit_qk(k):
                    h, j, own, first, last = items[k]
                    p = h // 2
                    sbank = k % 2
                    pbuf = k % 3
                    Sb = ps[sbank]
                    for ks in range(2):
                        kt = 2 * j + ks
                        qlo = 128 if (own and ks == 1) else 0
                        last_in_group = not (own or use_mask)
                        mm(Sb[:, ks * 256 + qlo: ks * 256 + 256], Kc[:, p, kt * 128:(kt + 1) * 128], Qz[:, h, qlo:T],
                           True, last_in_group, [("Kc", p, j), ("Qz", h)], [psb(sbank)])
                        if own:
                            mm(Sb[:, ks * 256 + qlo: ks * 256 + qlo + 128], ident, tri, False, True, ["cb"], [psb(sbank)])
                        elif use_mask:
                            mm(Sb[:, ks * 256: ks * 256 + 256], Wsel[:, j * 128:(j + 1) * 128], mbT[:, h, :], False, True,
                               ["cb", ("mbT", 0), ("mbT", 1)], [psb(sbank)])
                    if own:
                        sc.op("act", lambda e, Sb=Sb, pbuf=pbuf: e.activation(out=PT[:, pbuf, 0:256], in_=Sb[:, 0:256], func=AF.Exp),
                              [psb(sbank)], [("PT", pbuf)])
                        sc.op("act", lambda e, Sb=Sb, pbuf=pbuf: e.activation(out=PT[:, pbuf, 384:512], in_=Sb[:, 384:512], func=AF.Exp),
                              [psb(sbank)], [("PT", pbuf)])
                    else:
                        sc.op("act", lambda e, Sb=Sb, pbuf=pbuf: e.activation(out=PT[:, pbuf, :], in_=Sb[:, :], func=AF.Exp),
                              [psb(sbank)], [("PT", pbuf)])

                def emit_pv(k):
                    h, j, own, first, last = items[k]
                    p = h // 2
                    accb = 2 + (h % 2)
                    pbuf = k % 3
                    vsl = slice(p * 192, p * 192 + 128) if h % 2 == 0 else slice(p * 192 + 64, p * 192 + 192)
                    for ks in range(2):
                        kt = 2 * j + ks
                        qlo = 128 if (own and ks == 1) else 0
                        mm(ps[accb][:, qlo:T], Vc[:, kt, vsl], PT[:, pbuf, ks * 256 + qlo: ks * 256 + 256],
                           first and ks == 0, last and ks == 1, [("Vc", j), ("PT", pbuf)], [psb(accb)])
                    if last:
                        ab = h % 2
                        sc.op("act", lambda e, ab=ab, accb=accb: e.activation(out=accsb[:, ab, :], in_=ps[accb][:, 0:T], func=AF.Copy),
                              [psb(accb)], [("accsb", ab)])

                def emit_epi(h):
                    p = h // 2
                    ab = h % 2
                    mm(ps[4][:, 256:512], perm, accsb[:, ab, :], True, True, ["cf", ("accsb", ab)], [psb(4)])
                    rs = slice(0, 64) if h % 2 == 0 else slice(64, 128)
                    sc.op("dve", lambda e, rs=rs, ab=ab: e.reciprocal(out=rec[rs, ab, :], in_=ps[4][rs, 256:256 + T]),
                          [psb(4)], [("rec", ab)])
                    sc.op("dve", lambda e, rs=rs, ab=ab, p=p: e.tensor_tensor(out=catT[rs, 4 + p, :], in0=accsb[rs, ab, :], in1=rec[rs, ab, :], op=ALU.mult),
                          [("accsb", ab), ("rec", ab)], [("catT", 4 + p)])

                epi_due = {}
                emit_qk(0)
                for k in range(nit):
                    if k + 1 < nit:
                        emit_qk(k + 1)
                    emit_pv(k)
                    if items[k][4]:
                        epi_due.setdefault(min(k + (1 if i == 0 else 2), nit - 1), []).append(items[k][0])
                    for hh in epi_due.pop(k, []):
                        emit_epi(hh)
                for kk in sorted(epi_due):
                    for hh in epi_due[kk]:
                        emit_epi(hh)

                if STOP < 6:
                    return
                sc.tag = 'p6'
                token_major_proj(catT, lambda k: ("catT", k), "out", YB)
                if nxt is not None:
                    phase1(nxt[0], nxt[1], (g_tile + 1) % 2)
                    sc.tag = 'p6'
                layer_norm(hbuf, YB, 0)
                if nxt is not None:
                    phase2(nxt[0], nxt[1], (g_tile + 1) % 2, 0)
                    sc.tag = 'p6'

                transpose_to_hT(hb, lambda s: ("hb", s), hT, lambda s: ("hT", s), [0, 1])

                if STOP < 7:
                    return
                sc.tag = 'p7'
                for m in range(8):
                    slot, view = lhs_unit("xq", m)
                    if True:
                        bank = m % 2
                        for c in range(8):
                            mm(ps[bank][:, 0:T], view[:, c, :], hT[:, c, :], c == 0, c == 7,
                               hT_all + [("ring", slot)], [psb(bank)])
                        sc.op("act", lambda e, m=m, bank=bank: e.activation(out=XQ[:, m, :], in_=ps[bank][:, 0:T], func=AF.Copy, scale=1.0 / 16.0),
                              [psb(bank)], [("mbT", 0), ("mbT", 1)])
                def x_scores(xh):
                    sbank = xh % 2
                    pbuf = xh % 2
                    for ms in range(2):
                        for dc in range(2):
                            m = 2 * xh + dc
                            mm(ps[sbank][:, ms * 256: ms * 256 + 256], KmT[:, m, ms * 128:(ms + 1) * 128], XQ[:, m, :],
                               dc == 0, dc == 1, [("KmT", m), ("mbT", 0), ("mbT", 1)], [psb(sbank)])
                    sc.op("act", lambda e, sbank=sbank, pbuf=pbuf: e.activation(out=PxT[:, pbuf, :], in_=ps[sbank][:, :], func=AF.Exp),
                          [psb(sbank)], [("PxT", pbuf)])

                def x_pv(xh):
                    pbuf = xh % 2
                    ob = 2 + (xh % 2)
                    db = 4 if xh % 2 == 0 else 7
                    for dc in range(2):
                        for ms in range(2):
                            mm(ps[ob][:, dc * 256: dc * 256 + 256], Vm[:, ms, xh * 256 + dc * 128: xh * 256 + dc * 128 + 128],
                               PxT[:, pbuf, ms * 256: ms * 256 + 256], ms == 0, ms == 1, [("Vm", ms), ("PxT", pbuf)], [psb(ob)])
                    for ms in range(2):
                        mm(ps[db][:, 0:256], ones, PxT[:, pbuf, ms * 256: ms * 256 + 256], ms == 0, ms == 1, ["cb", ("PxT", pbuf)], [psb(db)])
                    rb = xh % 2
                    sc.op("dve", lambda e, rb=rb, db=db: e.reciprocal(out=rec[:, rb, :], in_=ps[db][:, 0:256]), [psb(db)], [("rec", rb)])
                    for dc in range(2):
                        sc.op("dve", lambda e, rb=rb, ob=ob, dc=dc, xh=xh: e.tensor_tensor(
                            out=catT[:, 2 * xh + dc, :], in0=ps[ob][:, dc * 256: dc * 256 + 256], in1=rec[:, rb, :], op=ALU.mult),
                            [psb(ob), ("rec", rb)], [("catT", 2 * xh + dc)])

                x_scores(0)
                for xh in range(4):
                    if xh + 1 < 4:
                        x_scores(xh + 1)
                    x_pv(xh)
                token_major_proj(catT, lambda k: ("catT", k), "xo", YB)
                layer_norm(hbuf, YB, 1)
                if nxt is not None:
                    phase2(nxt[0], nxt[1], (g_tile + 1) % 2, 1)
                    sc.tag = 'p7'
                transpose_to_hT(hb, lambda s: ("hb", s), hT, lambda s: ("hT", s), [0, 1])

                if STOP < 8:
                    return
                sc.tag = 'p8'
                pend = None
                actc = 0

                def down(pd):
                    f, ab_, dslot, dview, dk = pd
                    for s in range(2):
                        for half in range(2):
                            mm(ps[YB[s][half]][:, :], actT[:, ab_, s * 128:(s + 1) * 128], dview[:, dk, half * 512:(half + 1) * 512],
                               f == 0, f == NF - 1, [("actT", ab_), ("ring", dslot)], [psb(YB[s][half])])

                for f in range(NF):
                    if f == NF // 2 and nxt is not None:
                        phase34(nxt[1])
                        sc.tag = 'p8'
                    gslot, gview = lhs_unit("gate", f)
                    uslot, uview = lhs_unit("up", f)
                    dslot, dview = rhs_unit("down", f * 128, 1, 0, D)
                    fi = 0
                    if True:
                        gb = f % 2
                        ub = 4 if f % 2 == 0 else 7
                        for c in range(8):
                            mm(ps[gb][:, 0:T], gview[:, c, :], hT[:, c, :], c == 0, c == 7,
                               hT_all + [("ring", gslot)], [psb(gb)])
                        for c in range(8):
                            mm(ps[ub][:, 0:T], uview[:, c, :], hT[:, c, :], c == 0, c == 7,
                               hT_all + [("ring", uslot)], [psb(ub)])
                        if pend is not None:
                            down(pend)
                        sgb = f % 2
                        ab_ = actc % 3
                        actc += 1
                        sc.op("act", lambda e, gb=gb, sgb=sgb: e.activation(out=sg[:, sgb, :], in_=ps[gb][:, 0:T], func=AF.Silu),
                              [psb(gb)], [("sg", sgb)])
                        sc.op("dve", lambda e, ub=ub, sgb=sgb, ab_=ab_: e.tensor_tensor(out=actT[:, ab_, :], in0=sg[:, sgb, :], in1=ps[ub][:, 0:T], op=ALU.mult),
                              [psb(ub), ("sg", sgb)], [("actT", ab_)])
                        pend = (f, ab_, dslot, dview, fi)
                down(pend)
                layer_norm(hbuf, YB, 2, out_store=True)
            body()
            dst = out_d[b, i * T:(i + 1) * T, :].rearrange("(s p) d -> p s d", p=128)
            o = dma("pool", dst, hres[:, hbuf, :, :], [("hres", hbuf, 0), ("hres", hbuf, 1)], [], f"o{hbuf}")
            out_ops.append(o)
            g_tile += 1

    global LAST_SCHED
    LAST_SCHED = sc
    sc.finalize()
    finals = {}
    for o in out_ops:
        finals[o.key] = o
    sc.emit(nc, es, list(finals.values()))
    es.close()
    return nc


def _consts():
    bf = ml_dtypes.bfloat16
    ident = np.eye(128, dtype=np.float32)
    kk = np.arange(128)[:, None]
    qq = np.arange(128)[None, :]
    tri = np.where(kk > qq, NEG, 0.0).astype(np.float32)
    ones = np.ones((128, 128), np.float32)
    wsel = np.zeros((128, 2048), np.float32)
    for j in range(16):
        wsel[j, j * 128:(j + 1) * 128] = 1.0
    cbf = np.concatenate([ident, tri, ones, wsel], axis=1).astype(bf)
    perm = np.zeros((128, 128), np.float32)
    for p in range(128):
        perm[p, (p + 64) % 128] = 1.0
    inv = np.zeros((4, 16), np.float32)
    for g in range(4):
        w = 2 ** (g + 1)
        for t in range(16):
            inv[g, t] = 1.0 / min(t + 1, w)
    invb = np.broadcast_to(inv.reshape(1, 64), (128, 64))
    cf = np.concatenate([perm, invb, np.full((128, 1), -0.5, np.float32)], axis=1).astype(np.float32)
    return cbf, np.ascontiguousarray(cf)


def make_in_maps(inputs, ncores, NB):
    cbf, cf = _consts()
    f = lambda a: np.ascontiguousarray(np.asarray(a, dtype=np.float32))
    lnp = np.stack([np.broadcast_to(f(inputs[k])[0][None, :], (128, D))
                    for k in ("ln1_g", "ln1_b", "ln2_g", "ln2_b", "ln3_g", "ln3_b")]).astype(np.float32)
    pscale = np.ascontiguousarray(f(inputs["pool_scale"])[0].reshape(4, 128).T)
    shared = {
        "w_in": f(inputs["w_in"])[0], "w_pool": f(inputs["w_pool"])[0], "w_out": f(inputs["w_out"])[0],
        "w_xq": f(inputs["w_xq"])[0], "w_xkv": f(inputs["w_xkv"])[0], "w_xo": f(inputs["w_xo"])[0],
        "w_gate": f(inputs["w_gate"])[0], "w_up": f(inputs["w_up"])[0], "w_down": f(inputs["w_down"])[0],
        "lnp": np.ascontiguousarray(lnp), "pscale": pscale, "cbf": cbf, "cf32": cf,
    }
    x = f(inputs["x"])
    mem = f(inputs["mem"])
    maps = []
    for c in range(ncores):
        m = dict(shared)
        m["x"] = np.ascontiguousarray(x[c * NB:(c + 1) * NB])
        m["mem"] = np.ascontiguousarray(mem[c * NB:(c + 1) * NB])
        maps.append(m)
    return maps


def kernel(**inputs):
    x = np.asarray(inputs["x"])
    B, S, _ = x.shape
    ncores = 8
    NB = B // ncores
    nc = build(NB, S)
    maps = make_in_maps(inputs, ncores, NB)
    res = run_bass_kernel_spmd(nc, maps, core_ids=list(range(ncores)))
    out = np.concatenate([np.asarray(r["out"]) for r in res.results], axis=0)
    return out.astype(np.float32)
```

```python
import numpy as np
import ml_dtypes
from contextlib import ExitStack
import concourse.bass as bass
import concourse.mybir as mybir
from concourse.bass_utils import run_bass_kernel_spmd

F32 = mybir.dt.float32
BF16 = mybir.dt.bfloat16
AF = mybir.ActivationFunctionType
ALU = mybir.AluOpType
AX = mybir.AxisListType

D = 1024
DFF = 2816
NF = DFF // 128
NFP = NF // 2
MEM = 256
T = 256
ALPHA = 2.0 ** 0.25
EPS = 1e-5
NEG = -30000.0
NSLOT = 10
EPOCH_TILES = 4
import os
STOP = int(os.environ.get('KSTOP', '99'))


class Op:
    __slots__ = ("eng", "fn", "deps", "key", "kidx", "epoch", "signal", "cnt", "idx", "tag")


class Sched:
    ENGS = ("pe", "act", "dve", "pool", "sp")

    def __init__(self):
        self.ops = []
        self.last_w = {}
        self.readers = {}
        self.epoch = 0
        self.key_count = {}

    def op(self, eng, fn, reads=(), writes=(), key=None):
        o = Op()
        o.eng, o.fn, o.key, o.epoch = eng, fn, key, self.epoch
        o.tag = getattr(self, "tag", "")
        o.idx = len(self.ops)
        o.signal = False
        o.cnt = 0
        deps = set()
        psr = [r for r in reads if isinstance(r, tuple) and r[0] == "ps"]
        if psr:
            reads = [r for r in reads if r not in psr]
            writes = list(writes) + psr
        for r in reads:
            w = self.last_w.get(r)
            if w is not None:
                deps.add(w)
        for r in writes:
            w = self.last_w.get(r)
            if w is not None:
                deps.add(w)
            for rd in self.readers.get(r, ()):
                deps.add(rd)
        deps.discard(o.idx)
        o.deps = deps
        for r in reads:
            self.readers.setdefault(r, []).append(o.idx)
        for r in writes:
            self.last_w[r] = o.idx
            self.readers[r] = []
        if key is not None:
            self.key_count[key] = self.key_count.get(key, 0) + 1
            o.kidx = self.key_count[key]
        else:
            o.kidx = 0
        self.ops.append(o)
        return o

    def finalize(self):
        ops = self.ops
        for o in ops:
            for d in o.deps:
                a = ops[d]
                if a.key is None and not (a.eng == "pe" and o.eng == "pe"):
                    a.signal = True
        cnt = {}
        for o in ops:
            if o.key is None and o.signal:
                k = (o.eng, o.epoch)
                cnt[k] = cnt.get(k, 0) + 1
                o.cnt = cnt[k]
        self.nepoch = self.epoch + 1

    def emit(self, nc, es, final_waits):
        ops = self.ops
        engsem = {}
        for e in self.ENGS:
            for ep in range(self.nepoch):
                engsem[(e, ep)] = es.enter_context(nc.semaphore(f"s_{e}_{ep}"))
        keysem = {}
        for k in self.key_count:
            keysem[k] = es.enter_context(nc.semaphore(f"k_{k}"))
        per_eng = {e: [o for o in ops if o.eng == e] for e in self.ENGS}
        block = es.enter_context(nc.Block())

        def run_engine(ename, eng, extra_final=None):
            waited = {}
            maxep = {}
            for o in per_eng[ename]:
                need = {}
                for d in o.deps:
                    a = ops[d]
                    if a.key is not None:
                        sk = ("k", a.key)
                        need[sk] = max(need.get(sk, 0), 16 * a.kidx)
                    else:
                        if a.eng == "pe" and ename == "pe":
                            continue
                        if maxep.get(a.eng, -1) > a.epoch:
                            continue
                        sk = ("e", a.eng, a.epoch)
                        need[sk] = max(need.get(sk, 0), a.cnt)
                for sk, v in need.items():
                    if waited.get(sk, 0) >= v:
                        continue
                    waited[sk] = v
                    if sk[0] == "k":
                        eng.wait_ge(keysem[sk[1]], v)
                    else:
                        eng.wait_ge(engsem[(sk[1], sk[2])], v)
                        maxep[sk[1]] = max(maxep.get(sk[1], -1), sk[2])
                ins = o.fn(eng)
                if o.key is not None:
                    ins.then_inc(keysem[o.key], 16)
                elif o.signal:
                    ins.then_inc(engsem[(ename, o.epoch)], 1)
            if extra_final:
                for a in extra_final:
                    eng.wait_ge(keysem[a.key], 16 * a.kidx)

        @block.tensor
        def _(e):
            run_engine("pe", e)

        @block.scalar
        def _(e):
            run_engine("act", e)

        @block.vector
        def _(e):
            run_engine("dve", e)

        @block.gpsimd
        def _(e):
            run_engine("pool", e, extra_final=final_waits)

        @block.sync
        def _(e):
            run_engine("sp", e)


def build(NB, S):
    NBLK = S // T
    nc = bass.Bass("TRN2", target_bir_lowering=False)
    es = ExitStack()
    sc = Sched()

    def dram_in(name, shape, dt=F32):
        return nc.dram_tensor(name, list(shape), dt, kind="ExternalInput").ap()

    x_d = dram_in("x", [NB, S, D])
    mem_d = dram_in("mem", [NB, MEM, D])
    w_in_d = dram_in("w_in", [D, 2048])
    w_pool_d = dram_in("w_pool", [4, 128, 128])
    w_out_d = dram_in("w_out", [D, D])
    w_xq_d = dram_in("w_xq", [D, D])
    w_xkv_d = dram_in("w_xkv", [D, 2 * D])
    w_xo_d = dram_in("w_xo", [D, D])
    w_gate_d = dram_in("w_gate", [D, DFF])
    w_up_d = dram_in("w_up", [D, DFF])
    w_down_d = dram_in("w_down", [DFF, D])
    lnp_d = dram_in("lnp", [6, 128, D])
    pscale_d = dram_in("pscale", [128, 4])
    cb_d = dram_in("cbf", [128, 128 * 3 + 2048], BF16)
    cf_d = dram_in("cf32", [128, 128 + 64 + 1])
    out_d = nc.dram_tensor("out", [NB, S, D], F32, kind="ExternalOutput").ap()

    def dram_scr(name, shape):
        return nc.dram_tensor(name, list(shape), BF16, kind="Internal").ap()

    wb = {
        "in": dram_scr("wb_in", [D, 2048]),
        "out": dram_scr("wb_out", [D, D]),
        "xq": dram_scr("wb_xq", [D, D]),
        "xkv": dram_scr("wb_xkv", [D, 2 * D]),
        "xo": dram_scr("wb_xo", [D, D]),
        "gate": dram_scr("wb_gate", [D, DFF]),
        "up": dram_scr("wb_up", [D, DFF]),
        "down": dram_scr("wb_down", [DFF, D]),
    }
    wsrc = {"in": w_in_d, "out": w_out_d, "xq": w_xq_d, "xkv": w_xkv_d, "xo": w_xo_d,
            "gate": w_gate_d, "up": w_up_d, "down": w_down_d}

    def sb(name, shape, dt):
        return es.enter_context(nc.sbuf_tensor(name, list(shape), dt))

    ring = sb("ring", [128, NSLOT, 1024], BF16)
    Kc = sb("Kc", [128, 4, S], BF16)
    Vc = sb("Vc", [128, 2 * NBLK, 768], BF16)
    hres = sb("hres", [128, 2, 2, D], F32)
    hb = sb("hb", [128, 2, D], BF16)
    hT = sb("hT", [128, 8, T], BF16)
    xT = sb("xT", [128, 8, T], BF16)
    Qz = sb("Qz", [128, 8, T], BF16)
    catT = sb("catT", [128, 8, T], BF16)
    U = sb("U", [128, 4, 16 + T], F32)
    pa = sb("pa", [128, 16 + T], F32)
    pb = sb("pb", [128, 16 + T], F32)
    ptmp = sb("ptmp", [128, 16], F32)
    pooledT = sb("pooledT", [128, 4, T], BF16)
    wpool = sb("wpool", [128, 4, 128], BF16)
    PT = sb("PT", [128, 4, 512], BF16)
    km = sb("km", [128, 4, 16], BF16)
    kmf = sb("kmf", [128, 4], F32)
    gsc = sb("gsc", [128, 3, 8, 16], F32)
    gm = sb("gm", [128, 3, 8], F32)
    mb = sb("mb", [128, 8, 128], BF16)
    mbT = sb("mbT", [128, 8, T], BF16)
    XQ = mbT
    accsb = sb("accsb", [128, 2, T], F32)
    rec = sb("rec", [128, 2, T], F32)
    KmT = sb("KmT", [128, 8, MEM], BF16)
    Vm = sb("Vm", [128, 2, D], BF16)
    PxT = PT
    sg = sb("sg", [128, 2, T], F32)
    actT = sb("actT", [128, 3, T], BF16)
    lnp = sb("lnp_sb", [128, 6, D], F32)
    pscale = sb("pscale_sb", [128, 4], F32)
    cb = sb("cb", [128, 128 * 3 + 2048], BF16)
    cf = sb("cf", [128, 128 + 64 + 1], F32)
    stats = sb("stats", [128, 2, 2, 6], F32)
    mv = sb("mv", [128, 2, 2], F32)
    rstd = sb("rstd", [128, 2], F32)
    stage = sb("stage", [128, 2, 128], F32)

    ident = cb[:, 0:128]
    tri = cb[:, 128:256]
    ones = cb[:, 256:384]
    Wsel = cb[:, 384:384 + 2048]
    perm = cf[:, 0:128]
    invcnt = cf[:, 128:192]
    neghalf = cf[:, 192:193]

    ps = [es.enter_context(nc.psum_tensor(f"ps{b}", [128, 512], F32)) for b in range(8)]

    def psb(b):
        return ("ps", b)

    def ps_bf(b):
        return ps[b][:, :].bitcast(BF16)

    def dma(eng, out, in_, reads, writes, key):
        return sc.op(eng, lambda e: e.dma_start(out=out, in_=in_), reads, writes, key)

    dma("sp", cb[:, :], cb_d[:, :], [], ["cb"], "c0")
    dma("sp", cf[:, :], cf_d[:, :], [], ["cf"], "c1")
    dma("sp", lnp[:, :, :], lnp_d.rearrange("k p d -> p k d"), [], ["lnp"], "c2")
    dma("sp", pscale[:, :], pscale_d[:, :], [], ["pscale"], "c3")
    for nm in ("in", "out", "xq", "xkv", "xo", "gate", "up", "down"):
        src = wsrc[nm].rearrange("(p a) n -> p a n", p=128)
        dst = wb[nm].rearrange("(p a) n -> p a n", p=128)
        dma("pool", dst, src, [], [("wb", nm)], "wc_" + nm)
    wbU = {}
    wbU_done = set()
    for nm, ncol in (("in", 1536), ("xq", D), ("gate", DFF), ("up", DFF)):
        wbU[nm] = dram_scr("wbU_" + nm, [ncol // 128, 128, 1024])

    def ensure_relayout(nm):
        if nm in wbU_done:
            return
        wbU_done.add(nm)
        for m in range(wbU[nm].shape[0]):
            src = wb[nm][:, m * 128:(m + 1) * 128].rearrange("(c p) n -> p c n", p=128)
            dst = wbU[nm][m].rearrange("p (c n) -> p c n", n=128)
            dma("pool", dst, src, [("wb", nm)], [("wbU", nm, m), ("rukey", m % 4)], f"ru{m % 4}")

    ensure_relayout("in")
    for g in range(4):
        dma("sp", stage[:, g % 2, :], w_pool_d[g, :, :], [], [("stage", g % 2)], f"st{g % 2}")
        sc.op("act", lambda e, g=g: e.activation(out=wpool[:, g, :], in_=stage[:, g % 2, :], func=AF.Copy),
              [("stage", g % 2)], [("wpool", g)])
    sc.op("dve", lambda e: e.memset(Qz[:, :, :], 0.0), [], [("Qz", h) for h in range(8)])
    sc.op("dve", lambda e: e.memset(mb[:, :, :], 0.0), [], ["mb"])
    sc.op("dve", lambda e: e.memset(km[:, :, :], 0.0), [], ["km"])
    sc.op("pool", lambda e: e.memset(Vc[:, :, :], 1.0), [], [("Vc", j) for j in range(NBLK)])

    ring_ctr = [0]

    def stream(src_ap, shape_view, wname, res=None):
        u = ring_ctr[0]
        ring_ctr[0] += 1
        slot = u % NSLOT
        dst = shape_view(ring[:, slot, :])
        dma("sp", dst, src_ap, [res if res is not None else ("wb", wname)], [("ring", slot)], f"r{slot}")
        return slot

    def lhs_unit(wname, m):
        if wname in wbU:
            ensure_relayout(wname)
            slot = stream(wbU[wname][m], lambda r: r[:, 0:1024], wname, res=("wbU", wname, m))
        else:
            src = wb[wname][:, m * 128:(m + 1) * 128].rearrange("(c p) n -> p c n", p=128)
            slot = stream(src, lambda r: r[:, 0:1024].rearrange("p (c n) -> p c n", n=128), wname)
        view = ring[:, slot, 0:1024].rearrange("p (c n) -> p c n", n=128)
        return slot, view

    def rhs_unit(wname, row0, nrows_chunks, col0, ncols):
        assert nrows_chunks * ncols <= 1024
        src = wb[wname][row0:row0 + nrows_chunks * 128, col0:col0 + ncols].rearrange("(k p) n -> p k n", p=128)
        slot = stream(src, lambda r: r[:, 0:nrows_chunks * ncols].rearrange("p (k n) -> p k n", n=ncols), wname)
        view = ring[:, slot, 0:nrows_chunks * ncols].rearrange("p (k n) -> p k n", n=ncols)
        return slot, view

    def mm(out, lhsT, rhs, start, stop, reads, writes):
        return sc.op("pe", lambda e: e.matmul(out, lhsT, rhs, start=start, stop=stop), reads, writes)

    def transpose_to_hT(src_bf, src_res, dst, dst_res_fn, bank, evac="dve"):
        for s in range(2):
            pv = ps_bf(bank[s])
            for c in range(8):
                sc.op("pe", lambda e, s=s, c=c, pv=pv: e.transpose(pv[:, c * 128:(c + 1) * 128],
                                                                    src_bf[:, s, c * 128:(c + 1) * 128], ident),
                      [src_res(s), "cb"], [psb(bank[s])])
            if evac == "dve":
                sc.op("dve", lambda e, s=s, pv=pv: e.tensor_copy(
                    out=dst[:, :, s * 128:(s + 1) * 128],
                    in_=pv[:, :].rearrange("p (c t) -> p c t", t=128)),
                    [psb(bank[s])], [dst_res_fn(s)])
            else:
                sc.op("act", lambda e, s=s, pv=pv: e.activation(
                    out=dst[:, :, s * 128:(s + 1) * 128],
                    in_=pv[:, :].rearrange("p (c t) -> p c t", t=128), func=AF.Copy),
                    [psb(bank[s])], [dst_res_fn(s)])

    def layer_norm(hbuf, ybanks, k, out_store=None):
        for s in range(2):
            hr = hres[:, hbuf, s, :]
            for half in range(2):
                b = ybanks[s][half]
                sc.op("dve", lambda e, hr=hr, half=half, b=b: e.scalar_tensor_tensor(
                    out=hr[:, half * 512:(half + 1) * 512], in0=hr[:, half * 512:(half + 1) * 512],
                    scalar=ALPHA, in1=ps[b][:, :], op0=ALU.mult, op1=ALU.add),
                    [("hres", hbuf, s), psb(b)], [("hres", hbuf, s)])
            for half in range(2):
                sc.op("dve", lambda e, hr=hr, half=half, s=s: e.bn_stats(
                    out=stats[:, s, half, :], in_=hr[:, half * 512:(half + 1) * 512]),
                    [("hres", hbuf, s)], [("stats", s, half)])
            sc.op("dve", lambda e, s=s: e.bn_aggr(out=mv[:, s, :], in_=stats[:, s, :, :].rearrange("p a b -> p (a b)")),
                  [("stats", s, 0), ("stats", s, 1)], [("mv", s)])
            sc.op("dve", lambda e, s=s: e.tensor_scalar(out=rstd[:, s:s + 1], in0=mv[:, s, 1:2], scalar1=EPS,
                                                        scalar2=None, op0=ALU.add),
                  [("mv", s)], [("rstd", s)])
            sc.op("pool", lambda e, s=s: e.tensor_tensor(out=rstd[:, s:s + 1], in0=rstd[:, s:s + 1], in1=neghalf,
                                                         op=ALU.pow),
                  [("rstd", s), "cf"], [("rstd", s)])
            sc.op("dve", lambda e, hr=hr, s=s: e.tensor_scalar(out=hr, in0=hr, scalar1=mv[:, s, 0:1],
                                                               scalar2=rstd[:, s:s + 1], op0=ALU.subtract,
                                                               op1=ALU.mult),
                  [("hres", hbuf, s), ("mv", s), ("rstd", s)], [("hres", hbuf, s)])
            sc.op("pool", lambda e, hr=hr: e.tensor_tensor(out=hr, in0=hr, in1=lnp[:, 2 * k, :], op=ALU.mult),
                  [("hres", hbuf, s), "lnp"], [("hres", hbuf, s)])
            sc.op("dve", lambda e, hr=hr: e.tensor_tensor(out=hr, in0=hr, in1=lnp[:, 2 * k + 1, :], op=ALU.add),
                  [("hres", hbuf, s), "lnp"], [("hres", hbuf, s)])
            if out_store is None:
                sc.op("pool", lambda e, hr=hr, s=s: e.tensor_copy(out=hb[:, s, :], in_=hr),
                      [("hres", hbuf, s)], [("hb", s)])

    def token_major_proj(lhs_buf, lhs_res, wname, ybanks):
        for k in range(8):
            slot, view = rhs_unit(wname, k * 128, 1, 0, D)
            for s in range(2):
                for half in range(2):
                    mm(ps[ybanks[s][half]][:, :], lhs_buf[:, k, s * 128:(s + 1) * 128],
                       view[:, 0, half * 512:(half + 1) * 512], k == 0, k == 7,
                       [lhs_res(k), ("ring", slot)], [psb(ybanks[s][half])])

    YB = [[2, 3], [5, 6]]

    VB = [4, 7]
    SBANKS = [0, 1, 5]

    def phase1(b, i, hbuf):
        sc.tag = 'p1'
        for s in range(2):
            sc.op("act", lambda e, s=s, hbuf=hbuf: e.activation(out=hb[:, s, :], in_=hres[:, hbuf, s, :], func=AF.Copy),
                  [("hres", hbuf, s)], [("hb", s)])
        transpose_to_hT(hb, lambda s: ("hb", s), xT, lambda s: ("xT", s), [0, 1], evac="act")

    def phase2(b, i, hbuf):
        sc.tag = 'p2'
        for m in range(12):
            slot, view = lhs_unit("in", m)
            if True:
                bank = m % 2
                for c in range(8):
                    mm(ps[bank][:, 0:T], view[:, c, :], xT[:, c, :], c == 0, c == 7,
                       [("xT", 0), ("xT", 1), ("ring", slot)], [psb(bank)])
                if m < 4:
                    sc.op("act", lambda e, m=m, bank=bank: e.activation(out=U[:, m, 16:16 + T], in_=ps[bank][:, 0:T], func=AF.Copy),
                          [psb(bank)], [("U", m)])
                elif m < 8:
                    p = m - 4
                    sc.op("act", lambda e, p=p, bank=bank: e.activation(out=Qz[0:64, 2 * p, :], in_=ps[bank][0:64, 0:T], func=AF.Copy, scale=0.125),
                          [psb(bank)], [("Qz", 2 * p)])
                    sc.op("act", lambda e, p=p, bank=bank: e.activation(out=Qz[64:128, 2 * p + 1, :], in_=ps[bank][64:128, 0:T], func=AF.Copy, scale=0.125),
                          [psb(bank)], [("Qz", 2 * p + 1)])
                else:
                    p = m - 8
                    sc.op("act", lambda e, p=p, bank=bank, i=i: e.activation(
                        out=Kc[:, p, i * T:(i + 1) * T], in_=ps[bank][:, 0:T], func=AF.Copy, accum_out=kmf[:, p:p + 1]),
                        [psb(bank)], [("Kc", p, i), ("kmf", p)])
                    sc.op("pool", lambda e, p=p, i=i: e.tensor_scalar(out=km[:, p, i:i + 1], in0=kmf[:, p:p + 1], scalar1=1.0 / T,
                                                                      scalar2=None, op0=ALU.mult),
                          [("kmf", p)], ["km"])
        for u2 in range(4):
            slot, view = rhs_unit("in", u2 * 256, 2, 1536, 512)
            for kk in range(2):
                k = u2 * 2 + kk
                for s in range(2):
                    mm(ps[VB[s]][:, :], xT[:, k, s * 128:(s + 1) * 128], view[:, kk, :], k == 0, k == 7,
                       [("xT", s), ("ring", slot)], [psb(VB[s])])
        for s in range(2):
            kt = 2 * i + s
            vdst = Vc[:, kt, :].rearrange("p (pr x) -> p pr x", x=192)
            vsrc = ps[VB[s]][:, :].rearrange("p (pr two d) -> p pr two d", two=2, d=64)
            sc.op("act", lambda e, vdst=vdst, vsrc=vsrc: e.activation(out=vdst[:, :, 0:64], in_=vsrc[:, :, 0, :], func=AF.Copy),
                  [psb(VB[s])], [("Vc", i)])
            sc.op("act", lambda e, vdst=vdst, vsrc=vsrc: e.activation(out=vdst[:, :, 128:192], in_=vsrc[:, :, 1, :], func=AF.Copy),
                  [psb(VB[s])], [("Vc", i)])

    def phase34(i):
        if i == 0:
            sc.op("pool", lambda e: e.memset(U[:, :, 0:16], 0.0), [], [("U", g) for g in range(4)])
        sc.tag = 'p3'
        for g in range(4):
            src = U[:, g, :]
            bufs = [pa, pb]
            cur = src
            sh = 1
            lo = 0
            for lvl in range(g + 1):
                dst = bufs[lvl % 2]
                lo2 = lo + sh
                sc.op("pool", lambda e, dst=dst, cur=cur, sh=sh, lo2=lo2: e.tensor_tensor(
                    out=dst[:, lo2:16 + T], in0=cur[:, lo2:16 + T], in1=cur[:, lo2 - sh:16 + T - sh], op=ALU.add),
                    [("U", g), "pa", "pb"], ["pa" if lvl % 2 == 0 else "pb"])
                cur = dst
                lo = lo2
                sh *= 2
            w = 2 ** (g + 1)
            curname = "pa" if g % 2 == 0 else "pb"
            sc.op("dve", lambda e, cur=cur, g=g, w=w: e.scalar_tensor_tensor(
                out=pooledT[:, g, :], in0=cur[:, 16:16 + T], scalar=1.0 / w, in1=U[:, g, 16:16 + T],
                op0=ALU.mult, op1=ALU.subtract),
                [curname, ("U", g)], [("pooledT", g)])
            if i == 0:
                sc.op("pool", lambda e, cur=cur, g=g: e.tensor_tensor(
                    out=ptmp[:, :], in0=cur[:, 16:32], in1=invcnt[:, g * 16:(g + 1) * 16], op=ALU.mult),
                    [curname, "cf"], ["ptmp"])
                sc.op("pool", lambda e, g=g: e.tensor_tensor(
                    out=pooledT[:, g, 0:16], in0=ptmp[:, :], in1=U[:, g, 16:32], op=ALU.subtract),
                    ["ptmp", ("U", g)], [("pooledT", g)])
            sc.op("pool", lambda e, g=g: e.tensor_copy(out=U[:, g, 0:16], in_=U[:, g, T:T + 16]),
                  [("U", g), curname], [("U", g)])
            bank = g % 2
            mm(ps[bank][:, 0:T], wpool[:, g, :], pooledT[:, g, :], True, True,
               [("wpool", g), ("pooledT", g)], [psb(bank)])
            sc.op("dve", lambda e, g=g, bank=bank: e.tensor_scalar(out=catT[:, g, :], in0=ps[bank][:, 0:T],
                                                                  scalar1=pscale[:, g:g + 1], scalar2=None, op0=ALU.mult),
                  [psb(bank), "pscale"], [("catT", g)])

        sc.tag = 'p4'
        use_mask = i > 3
        if use_mask:
            for s in range(2):
                for h in range(8):
                    mm(ps[4][:, s * 128 + h * 16: s * 128 + h * 16 + 16], Qz[:, h, s * 128:(s + 1) * 128],
                       km[:, h // 2, :], True, True, [("Qz", h), "km"], [psb(4)])
                G = ps[4][:, s * 128:(s + 1) * 128].rearrange("p (h n) -> p h n", n=16)[:, :, 0:i]
                A = gsc[:, 0, :, 0:i]
                B = gsc[:, 1, :, 0:i]
                C = gsc[:, 2, :, 0:i]

                def bc(k_, i=i):
                    return gm[:, k_, :].unsqueeze(2).broadcast_to([128, 8, i])

                gr = ["gsc", "gm", psb(4)]
                sc.op("dve", lambda e, G=G: e.tensor_reduce(out=gm[:, 0, :], in_=G, axis=AX.X, op=ALU.max), gr, ["gm"])
                sc.op("dve", lambda e, G=G, A=A, bc=bc: e.tensor_tensor(out=A, in0=G, in1=bc(0), op=ALU.is_ge), gr, ["gsc"])
                sc.op("dve", lambda e, G=G, A=A, B=B: e.scalar_tensor_tensor(out=B, in0=A, scalar=-1e30, in1=G, op0=ALU.mult, op1=ALU.add), gr, ["gsc"])
                sc.op("dve", lambda e, B=B: e.tensor_reduce(out=gm[:, 1, :], in_=B, axis=AX.X, op=ALU.max), gr, ["gm"])
                sc.op("dve", lambda e, A=A, B=B, bc=bc: e.tensor_tensor(out=A, in0=B, in1=bc(1), op=ALU.is_ge), gr, ["gsc"])
                sc.op("dve", lambda e, A=A, B=B, C=C: e.scalar_tensor_tensor(out=C, in0=A, scalar=-1e30, in1=B, op0=ALU.mult, op1=ALU.add), gr, ["gsc"])
                sc.op("dve", lambda e, C=C: e.tensor_reduce(out=gm[:, 2, :], in_=C, axis=AX.X, op=ALU.max), gr, ["gm"])
                sc.op("dve", lambda e, G=G, A=A, bc=bc: e.tensor_tensor(out=A, in0=G, in1=bc(2), op=ALU.is_ge), gr, ["gsc"])
                sc.op("dve", lambda e, A=A, i=i: e.tensor_scalar(out=mb[:, :, 0:i], in0=A, scalar1=-NEG, scalar2=NEG,
                                                               op0=ALU.mult, op1=ALU.add), gr, ["mb"])
                pv = ps_bf(7)
                for h in range(8):
                    sc.op("pe", lambda e, h=h, pv=pv: e.transpose(pv[:, h * 128:(h + 1) * 128], mb[:, h, :], ident),
                          ["mb", "cb"], [psb(7)])
                sc.op("act", lambda e, s=s, pv=pv: e.activation(
                    out=mbT[:, :, s * 128:(s + 1) * 128], in_=pv[:, :].rearrange("p (h t) -> p h t", t=128), func=AF.Copy),
                    [psb(7)], [("mbT", s)])

    out_ops = []
    g_tile = 0

    def load_x(b, i, hbuf):
        src = x_d[b, i * T:(i + 1) * T, :].rearrange("(s p) d -> p s d", p=128)
        dma("pool", hres[:, hbuf, :, :], src, [], [("hres", hbuf, 0), ("hres", hbuf, 1)], f"x{hbuf}")

    load_x(0, 0, 0)
    sc.tag = 'p1'
    P12_EARLY = True
    for b in range(NB):
        sc.epoch = (g_tile // EPOCH_TILES) + 1 if g_tile > 0 else 0
        sc.tag = 'mem'
        mbuf = (g_tile + 1) % 2
        if STOP >= 1:
            dma("pool", hres[:, mbuf, :, :], mem_d[b, :, :].rearrange("(s p) d -> p s d", p=128),
                [], [("hres", mbuf, 0), ("hres", mbuf, 1)], f"x{mbuf}")
            for s in range(2):
                sc.op("act", lambda e, s=s, mbuf=mbuf: e.activation(out=hb[:, s, :], in_=hres[:, mbuf, s, :], func=AF.Copy),
                      [("hres", mbuf, s)], [("hb", s)])
            transpose_to_hT(hb, lambda s: ("hb", s), hT, lambda s: ("hT", s), [0, 1])
            hT_all = [("hT", 0), ("hT", 1)]
            for m in range(8):
                slot, view = lhs_unit("xkv", m)
                if True:
                    bank = m % 2
                    for c in range(8):
                        mm(ps[bank][:, 0:MEM], view[:, c, :], hT[:, c, :], c == 0, c == 7,
                           hT_all + [("ring", slot)], [psb(bank)])
                    sc.op("act", lambda e, m=m, bank=bank: e.activation(out=KmT[:, m, :], in_=ps[bank][:, 0:MEM], func=AF.Copy),
                          [psb(bank)], [("KmT", m)])
            for k in range(8):
                slot, view = rhs_unit("xkv", k * 128, 1, D, D)
                if True:
                    for s in range(2):
                        for half in range(2):
                            mm(ps[YB[s][half]][:, :], hT[:, k, s * 128:(s + 1) * 128],
                               view[:, 0, half * 512:(half + 1) * 512], k == 0, k == 7,
                               [("hT", s), ("ring", slot)], [psb(YB[s][half])])
            for s in range(2):
                for half in range(2):
                    sc.op("act", lambda e, s=s, half=half: e.activation(
                        out=Vm[:, s, half * 512:(half + 1) * 512], in_=ps[YB[s][half]][:, :], func=AF.Copy),
                        [psb(YB[s][half])], [("Vm", s)])

        for i in range(NBLK):
            sc.epoch = g_tile // EPOCH_TILES + 1
            hbuf = g_tile % 2
            nxt = (b, i + 1) if i + 1 < NBLK else ((b + 1, 0) if b + 1 < NB else None)

            if nxt is not None:
                load_x(nxt[0], nxt[1], (g_tile + 1) % 2)

            if g_tile == 0:
                sc.tag = 'p2'
                phase1(b, i, hbuf)
                phase2(b, i, hbuf)
                phase34(i)

            def body():
                if STOP < 5:
                    return
                sc.tag = 'p5'
                use_mask = i > 3
                items = []
                for h in range(8):
                    blocks = [(i, True)] + [(j, False) for j in range(i)]
                    for bi, (j, own) in enumerate(blocks):
                        items.append((h, j, own, bi == 0, bi == len(blocks) - 1))
                nit = len(items)

                def emit_qk(k):
                    h, j, own, first, last = items[k]
                    p = h // 2
                    sbank = SBANKS[k % 3]
                    pbuf = k % 4
                    Sb = ps[sbank]
                    for ks in range(2):
                        kt = 2 * j + ks
                        qlo = 128 if (own and ks == 1) else 0
                        last_in_group = not (own or use_mask)
                        mm(Sb[:, ks * 256 + qlo: ks * 256 + 256], Kc[:, p, kt * 128:(kt + 1) * 128], Qz[:, h, qlo:T],
                           True, last_in_group, [("Kc", p, j), ("Qz", h)], [psb(sbank)])
                        if own:
                            mm(Sb[:, ks * 256 + qlo: ks * 256 + qlo + 128], ident, tri, False, True, ["cb"], [psb(sbank)])
                        elif use_mask:
                            mm(Sb[:, ks * 256: ks * 256 + 256], Wsel[:, j * 128:(j + 1) * 128], mbT[:, h, :], False, True,
                               ["cb", ("mbT", 0), ("mbT", 1)], [psb(sbank)])
                    if own:
                        sc.op("act", lambda e, Sb=Sb, pbuf=pbuf: e.activation(out=PT[:, pbuf, 0:256], in_=Sb[:, 0:256], func=AF.Exp),
                              [psb(sbank)], [("PT", pbuf)])
                        sc.op("act", lambda e, Sb=Sb, pbuf=pbuf: e.activation(out=PT[:, pbuf, 384:512], in_=Sb[:, 384:512], func=AF.Exp),
                              [psb(sbank)], [("PT", pbuf)])
                    else:
                        sc.op("act", lambda e, Sb=Sb, pbuf=pbuf: e.activation(out=PT[:, pbuf, :], in_=Sb[:, :], func=AF.Exp),
                              [psb(sbank)], [("PT", pbuf)])

                def emit_pv(k):
                    h, j, own, first, last = items[k]
                    p = h // 2
                    accb = 2 + (h % 2)
                    pbuf = k % 4
                    vsl = slice(p * 192, p * 192 + 128) if h % 2 == 0 else slice(p * 192 + 64, p * 192 + 192)
                    for ks in range(2):
                        kt = 2 * j + ks
                        qlo = 128 if (own and ks == 1) else 0
                        mm(ps[accb][:, qlo:T], Vc[:, kt, vsl], PT[:, pbuf, ks * 256 + qlo: ks * 256 + 256],
                           first and ks == 0, last and ks == 1, [("Vc", j), ("PT", pbuf)], [psb(accb)])
                    if last:
                        ab = h % 2
                        sc.op("act", lambda e, ab=ab, accb=accb: e.activation(out=accsb[:, ab, :], in_=ps[accb][:, 0:T], func=AF.Copy),
                              [psb(accb)], [("accsb", ab)])

                def emit_epi(h):
                    p = h // 2
                    ab = h % 2
                    mm(ps[4][:, 256:512], perm, accsb[:, ab, :], True, True, ["cf", ("accsb", ab)], [psb(4)])
                    rs = slice(0, 64) if h % 2 == 0 else slice(64, 128)
                    sc.op("dve", lambda e, rs=rs, ab=ab: e.reciprocal(out=rec[rs, ab, :], in_=ps[4][rs, 256:256 + T]),
                          [psb(4)], [("rec", ab)])
                    sc.op("dve", lambda e, rs=rs, ab=ab, p=p: e.tensor_tensor(out=catT[rs, 4 + p, :], in0=accsb[rs, ab, :], in1=rec[rs, ab, :], op=ALU.mult),
                          [("accsb", ab), ("rec", ab)], [("catT", 4 + p)])

                epi_due = {}
                emit_qk(0)
                if nit > 1:
                    emit_qk(1)
                for k in range(nit):
                    if k + 2 < nit:
                        emit_qk(k + 2)
                    emit_pv(k)
                    if items[k][4]:
                        epi_due.setdefault(min(k + (1 if i == 0 else 2), nit - 1), []).append(items[k][0])
                    for hh in epi_due.pop(k, []):
                        emit_epi(hh)
                for kk in sorted(epi_due):
                    for hh in epi_due[kk]:
                        emit_epi(hh)

                if STOP < 6:
                    return
                sc.tag = 'p6'
                token_major_proj(catT, lambda k: ("catT", k), "out", YB)
                if nxt is not None:
                    phase1(nxt[0], nxt[1], (g_tile + 1) % 2)
                    sc.tag = 'p6'
                layer_norm(hbuf, YB, 0)
                if nxt is not None:
                    phase2(nxt[0], nxt[1], (g_tile + 1) % 2)
                    sc.tag = 'p6'

                transpose_to_hT(hb, lambda s: ("hb", s), hT, lambda s: ("hT", s), [0, 1])

                if STOP < 7:
                    return
                sc.tag = 'p7'
                for m in range(8):
                    slot, view = lhs_unit("xq", m)
                    if True:
                        bank = m % 2
                        for c in range(8):
                            mm(ps[bank][:, 0:T], view[:, c, :], hT[:, c, :], c == 0, c == 7,
                               hT_all + [("ring", slot)], [psb(bank)])
                        sc.op("act", lambda e, m=m, bank=bank: e.activation(out=XQ[:, m, :], in_=ps[bank][:, 0:T], func=AF.Copy, scale=1.0 / 16.0),
                              [psb(bank)], [("mbT", 0), ("mbT", 1)])
                def x_scores(xh):
                    sbank = xh % 2
                    pbuf = xh % 2
                    for ms in range(2):
                        for dc in range(2):
                            m = 2 * xh + dc
                            mm(ps[sbank][:, ms * 256: ms * 256 + 256], KmT[:, m, ms * 128:(ms + 1) * 128], XQ[:, m, :],
                               dc == 0, dc == 1, [("KmT", m), ("mbT", 0), ("mbT", 1)], [psb(sbank)])
                    sc.op("act", lambda e, sbank=sbank, pbuf=pbuf: e.activation(out=PxT[:, pbuf, :], in_=ps[sbank][:, :], func=AF.Exp),
                          [psb(sbank)], [("PT", pbuf)])

                def x_pv(xh):
                    pbuf = xh % 2
                    ob = 2 + (xh % 2)
                    db = 4 if xh % 2 == 0 else 7
                    for dc in range(2):
                        for ms in range(2):
                            mm(ps[ob][:, dc * 256: dc * 256 + 256], Vm[:, ms, xh * 256 + dc * 128: xh * 256 + dc * 128 + 128],
                               PxT[:, pbuf, ms * 256: ms * 256 + 256], ms == 0, ms == 1, [("Vm", ms), ("PT", pbuf)], [psb(ob)])
                    for ms in range(2):
                        mm(ps[db][:, 0:256], ones, PxT[:, pbuf, ms * 256: ms * 256 + 256], ms == 0, ms == 1, ["cb", ("PT", pbuf)], [psb(db)])
                    rb = xh % 2
                    sc.op("dve", lambda e, rb=rb, db=db: e.reciprocal(out=rec[:, rb, :], in_=ps[db][:, 0:256]), [psb(db)], [("rec", rb)])
                    for dc in range(2):
                        sc.op("dve", lambda e, rb=rb, ob=ob, dc=dc, xh=xh: e.tensor_tensor(
                            out=catT[:, 2 * xh + dc, :], in0=ps[ob][:, dc * 256: dc * 256 + 256], in1=rec[:, rb, :], op=ALU.mult),
                            [psb(ob), ("rec", rb)], [("catT", 2 * xh + dc)])

                x_scores(0)
                for xh in range(4):
                    if xh + 1 < 4:
                        x_scores(xh + 1)
                    x_pv(xh)
                token_major_proj(catT, lambda k: ("catT", k), "xo", YB)
                layer_norm(hbuf, YB, 1)
                transpose_to_hT(hb, lambda s: ("hb", s), hT, lambda s: ("hT", s), [0, 1])

                if STOP < 8:
                    return
                sc.tag = 'p8'
                pend = None
                actc = 0

                def down(pd):
                    f, ab_, dslot, dview, dk = pd
                    for s in range(2):
                        for half in range(2):
                            mm(ps[YB[s][half]][:, :], actT[:, ab_, s * 128:(s + 1) * 128], dview[:, dk, half * 512:(half + 1) * 512],
                               f == 0, f == NF - 1, [("actT", ab_), ("ring", dslot)], [psb(YB[s][half])])

                for f in range(NF):
                    if f == NF // 2 and nxt is not None:
                        phase34(nxt[1])
                        sc.tag = 'p8'
                    gslot, gview = lhs_unit("gate", f)
                    uslot, uview = lhs_unit("up", f)
                    dslot, dview = rhs_unit("down", f * 128, 1, 0, D)
                    fi = 0
                    if True:
                        gb = f % 2
                        ub = 4 if f % 2 == 0 else 7
                        for c in range(8):
                            mm(ps[gb][:, 0:T], gview[:, c, :], hT[:, c, :], c == 0, c == 7,
                               hT_all + [("ring", gslot)], [psb(gb)])
                        for c in range(8):
                            mm(ps[ub][:, 0:T], uview[:, c, :], hT[:, c, :], c == 0, c == 7,
                               hT_all + [("ring", uslot)], [psb(ub)])
                        if pend is not None:
                            down(pend)
                        sgb = f % 2
                        ab_ = actc % 3
                        actc += 1
                        sc.op("act", lambda e, gb=gb, sgb=sgb: e.activation(out=sg[:, sgb, :], in_=ps[gb][:, 0:T], func=AF.Silu),
                              [psb(gb)], [("sg", sgb)])
                        sc.op("dve", lambda e, ub=ub, sgb=sgb, ab_=ab_: e.tensor_tensor(out=actT[:, ab_, :], in0=sg[:, sgb, :], in1=ps[ub][:, 0:T], op=ALU.mult),
                              [psb(ub), ("sg", sgb)], [("actT", ab_)])
                        pend = (f, ab_, dslot, dview, fi)
                down(pend)
                layer_norm(hbuf, YB, 2, out_store=True)
            body()
            dst = out_d[b, i * T:(i + 1) * T, :].rearrange("(s p) d -> p s d", p=128)
            o = dma("pool", dst, hres[:, hbuf, :, :], [("hres", hbuf, 0), ("hres", hbuf, 1)], [], f"o{hbuf}")
            out_ops.append(o)
            g_tile += 1

    global LAST_SCHED
    LAST_SCHED = sc
    sc.finalize()
    finals = {}
    for o in out_ops:
        finals[o.key] = o
    sc.emit(nc, es, list(finals.values()))
    es.close()
    return nc


def _consts():
    bf = ml_dtypes.bfloat16
    ident = np.eye(128, dtype=np.float32)
    kk = np.arange(128)[:, None]
    qq = np.arange(128)[None, :]
    tri = np.where(kk > qq, NEG, 0.0).astype(np.float32)
    ones = np.ones((128, 128), np.float32)
    wsel = np.zeros((128, 2048), np.float32)
    for j in range(16):
        wsel[j, j * 128:(j + 1) * 128] = 1.0
    cbf = np.concatenate([ident, tri, ones, wsel], axis=1).astype(bf)
    perm = np.zeros((128, 128), np.float32)
    for p in range(128):
        perm[p, (p + 64) % 128] = 1.0
    inv = np.zeros((4, 16), np.float32)
    for g in range(4):
        w = 2 ** (g + 1)
        for t in range(16):
            inv[g, t] = 1.0 / min(t + 1, w)
    invb = np.broadcast_to(inv.reshape(1, 64), (128, 64))
    cf = np.concatenate([perm, invb, np.full((128, 1), -0.5, np.float32)], axis=1).astype(np.float32)
    return cbf, np.ascontiguousarray(cf)


def make_in_maps(inputs, ncores, NB):
    cbf, cf = _consts()
    f = lambda a: np.ascontiguousarray(np.asarray(a, dtype=np.float32))
    lnp = np.stack([np.broadcast_to(f(inputs[k])[0][None, :], (128, D))
                    for k in ("ln1_g", "ln1_b", "ln2_g", "ln2_b", "ln3_g", "ln3_b")]).astype(np.float32)
    pscale = np.ascontiguousarray(f(inputs["pool_scale"])[0].reshape(4, 128).T)
    shared = {
        "w_in": f(inputs["w_in"])[0], "w_pool": f(inputs["w_pool"])[0], "w_out": f(inputs["w_out"])[0],
        "w_xq": f(inputs["w_xq"])[0], "w_xkv": f(inputs["w_xkv"])[0], "w_xo": f(inputs["w_xo"])[0],
        "w_gate": f(inputs["w_gate"])[0], "w_up": f(inputs["w_up"])[0], "w_down": f(inputs["w_down"])[0],
        "lnp": np.ascontiguousarray(lnp), "pscale": pscale, "cbf": cbf, "cf32": cf,
    }
    x = f(inputs["x"])
    mem = f(inputs["mem"])
    maps = []
    for c in range(ncores):
        m = dict(shared)
        m["x"] = np.ascontiguousarray(x[c * NB:(c + 1) * NB])
        m["mem"] = np.ascontiguousarray(mem[c * NB:(c + 1) * NB])
        maps.append(m)
    return maps


def kernel(**inputs):
    x = np.asarray(inputs["x"])
    B, S, _ = x.shape
    ncores = 8
    NB = B // ncores
    nc = build(NB, S)
    maps = make_in_maps(inputs, ncores, NB)
    res = run_bass_kernel_spmd(nc, maps, core_ids=list(range(ncores)))
    out = np.concatenate([np.asarray(r["out"]) for r in res.results], axis=0)
    return out.astype(np.float32)
```
